# Optimizing a Trainium2 kernel written in Bass

```python
import math
import jax, jax.numpy as jnp
from jax import lax
import numpy as np

D_MODEL = 1024
BATCH = 2
SEQ = 16384
DEPTH = 2

N_A_LAYERS = DEPTH // 2
N_B_LAYERS = DEPTH - N_A_LAYERS
D_FF = 2816
A_HEADS = 8
A_KEY_DIM = 128
A_VAL_DIM = D_MODEL // A_HEADS
A_FORGET_DIM = A_HEADS * A_KEY_DIM
A_VAL_WIDTH = A_HEADS * A_VAL_DIM
A_IN_WIDTH = 2 * A_FORGET_DIM + 2 * A_VAL_WIDTH
A_CHUNK = 64
B_WINDOWS = (128, 512, 2048)
B_DILATIONS = (1, 4, 16)
B_GROUPS = 3
B_HEADS = 16
B_HEAD_DIM = D_MODEL // B_HEADS
B_QKV_WIDTH = B_GROUPS * B_HEADS * B_HEAD_DIM
B_BLOCK = 128
NUM_BUCKETS = 32
MAX_DISTANCE = 2048
EPS = 1e-6

kernel_name = 'yoco_hgrn2_dilated_macaron'


def _rms(x, gain):
    xf = x.astype(jnp.float32)
    y = xf * lax.rsqrt(jnp.mean(xf * xf, axis=-1, keepdims=True) + EPS)
    return (y * gain.astype(jnp.float32)).astype(x.dtype)


def _swiglu(h, w_in, w_out):
    gate, up = jnp.split(h @ w_in, 2, axis=-1)
    return (jax.nn.silu(gate) * up) @ w_out


def _lower_bounds(lb_logits):
    p = jax.nn.softmax(lb_logits.astype(jnp.float32), axis=0)
    return jnp.cumsum(p, axis=0)[:-1]


def _hgrn2(h, w_in, lb, out_gain, w_out):
    bsz, seq, _ = h.shape
    n_chunks = seq // A_CHUNK
    f32 = jnp.float32
    q, f, i, g = jnp.split(h @ w_in, [A_FORGET_DIM, 2 * A_FORGET_DIM, 2 * A_FORGET_DIM + A_VAL_WIDTH], axis=-1)
    log_f = jnp.logaddexp(jnp.log(lb), jnp.log1p(-lb) + jax.nn.log_sigmoid(f.astype(f32)))
    k = -jnp.expm1(log_f)
    q = jax.nn.silu(q.astype(f32))
    v = i.astype(f32)

    def to_chunks(t, dh):
        return t.reshape(bsz, n_chunks, A_CHUNK, A_HEADS, dh).transpose(1, 0, 3, 2, 4)

    qc, kc, gc = (to_chunks(t, A_KEY_DIM) for t in (q, k, log_f))
    vc = to_chunks(v, A_VAL_DIM)
    causal = jnp.tril(jnp.ones((A_CHUNK, A_CHUNK), dtype=bool))

    def step(state, inp):
        qb, kb, vb, gb = inp
        cum = jnp.cumsum(gb, axis=2)
        rel = cum[:, :, :, None, :] - cum[:, :, None, :, :]
        decay = jnp.exp(jnp.where(causal[:, :, None], rel, -jnp.inf))
        scores = jnp.einsum('bhtd,bhtsd,bhsd->bhts', qb, decay, kb)
        out = (jnp.einsum('bhts,bhse->bhte', scores, vb)
               + jnp.einsum('bhtd,bhde->bhte', qb * jnp.exp(cum), state))
        last = cum[:, :, -1, :]
        state = (jnp.exp(last)[..., None] * state
                 + jnp.einsum('bhsd,bhse->bhde', kb * jnp.exp(last[:, :, None, :] - cum), vb))
        return state, out

    state0 = jnp.zeros((bsz, A_HEADS, A_KEY_DIM, A_VAL_DIM), f32)
    _, o = lax.scan(step, state0, (qc, kc, vc, gc))
    o = o.transpose(1, 0, 3, 2, 4).reshape(bsz, seq, A_HEADS, A_VAL_DIM)
    o = _rms(o, out_gain) * jax.nn.silu(g.astype(f32)).reshape(bsz, seq, A_HEADS, A_VAL_DIM)
    return o.reshape(bsz, seq, A_VAL_WIDTH).astype(h.dtype) @ w_out


def _t5_bucket(dist):
    dist = np.asarray(dist, np.int32)
    max_exact = NUM_BUCKETS // 2
    large = max_exact + (np.log(np.maximum(dist, 1) / max_exact)
                         / math.log(MAX_DISTANCE / max_exact) * (NUM_BUCKETS - max_exact)).astype(np.int32)
    large = np.minimum(large, NUM_BUCKETS - 1)
    return np.where(dist < max_exact, dist, large).astype(np.int32)


def _band_static(window, dilation):
    nq = B_BLOCK // dilation
    wd = window // dilation
    nj = wd + nq
    m = np.arange(nj)[None, :] - np.arange(nq)[:, None]
    band = (m >= 0) & (m <= wd)
    bucket = _t5_bucket((wd - np.clip(m, 0, wd)) * dilation)
    return band, bucket


def _shared_kv(x, kv_norm, w_kv, k_gain):
    bsz, seq, _ = x.shape
    kv = (_rms(x, kv_norm) @ w_kv).reshape(bsz, seq, 2, B_GROUPS, B_HEADS, B_HEAD_DIM)
    k = _rms(kv[:, :, 0], k_gain[:, None, :])
    v = kv[:, :, 1]
    k_pads = [jnp.pad(k[:, :, g], ((0, 0), (w, 0), (0, 0), (0, 0))) for g, w in enumerate(B_WINDOWS)]
    v_pads = [jnp.pad(v[:, :, g], ((0, 0), (w, 0), (0, 0), (0, 0))) for g, w in enumerate(B_WINDOWS)]
    return k_pads, v_pads


def _dilated_attention(h, w_q, q_gain, w_o, k_pads, v_pads, rel_bias):
    bsz, seq, _ = h.shape
    n_blocks = seq // B_BLOCK
    f32 = jnp.float32
    q = (h @ w_q).reshape(bsz, seq, B_GROUPS, B_HEADS, B_HEAD_DIM)
    q = _rms(q, q_gain[:, None, :]) * (B_HEAD_DIM ** -0.5)
    q_groups = [q[:, :, g] for g in range(B_GROUPS)]
    statics = [_band_static(w, d) for w, d in zip(B_WINDOWS, B_DILATIONS)]
    biases = [rel_bias[bucket][..., g * B_HEADS:(g + 1) * B_HEADS].transpose(2, 0, 1).astype(f32)
              for g, (_, bucket) in enumerate(statics)]

    def block(b):
        start = b * B_BLOCK
        outs, lses = [], []
        for g, (window, dil) in enumerate(zip(B_WINDOWS, B_DILATIONS)):
            band = statics[g][0]
            nq = B_BLOCK // dil
            nj = window // dil + nq
            qb = lax.dynamic_slice_in_dim(q_groups[g], start, B_BLOCK, axis=1).reshape(bsz, nq, dil, B_HEADS, B_HEAD_DIM)
            kb = lax.dynamic_slice_in_dim(k_pads[g], start, window + B_BLOCK, axis=1).reshape(bsz, nj, dil, B_HEADS, B_HEAD_DIM)
            vb = lax.dynamic_slice_in_dim(v_pads[g], start, window + B_BLOCK, axis=1).reshape(bsz, nj, dil, B_HEADS, B_HEAD_DIM)
            s = jnp.einsum('birhd,bjrhd->bhrij', qb, kb).astype(f32) + biases[g][None, :, None]
            pos = start - window + jnp.arange(nj)[:, None] * dil + jnp.arange(dil)[None, :]
            valid = band[None, :, :] & (pos.T >= 0)[:, None, :]
            s = jnp.where(valid[None, None], s, -jnp.inf)
            lse = jax.nn.logsumexp(s, axis=-1)
            p = jnp.exp(s - lse[..., None]).astype(vb.dtype)
            o = jnp.einsum('bhrij,bjrhd->birhd', p, vb).reshape(bsz, B_BLOCK, B_HEADS, B_HEAD_DIM)
            outs.append(o.astype(f32))
            lses.append(lse.transpose(0, 3, 2, 1).reshape(bsz, B_BLOCK, B_HEADS))
        wts = jax.nn.softmax(jnp.stack(lses), axis=0)
        o = jnp.einsum('gbqh,gbqhd->bqhd', wts, jnp.stack(outs))
        return o.reshape(bsz, B_BLOCK, B_HEADS * B_HEAD_DIM).astype(h.dtype)

    o = lax.map(block, jnp.arange(n_blocks))
    o = o.transpose(1, 0, 2, 3).reshape(bsz, seq, B_HEADS * B_HEAD_DIM)
    return o @ w_o


def setup_inputs(seed: int = 0) -> dict:
    key = jax.random.key(seed)
    ks = jax.random.split(key, 16)

    def dense(k, shape, fan_in):
        return jax.random.normal(k, shape, jnp.float32) * fan_in ** -0.5

    def gain(k, shape):
        return 1.0 + 0.1 * jax.random.normal(k, shape, jnp.float32)

    return {
        'x': jax.random.normal(ks[0], (BATCH, SEQ, D_MODEL), jnp.float32),
        'norm_gain': gain(ks[1], (DEPTH, 3, D_MODEL)),
        'ffn_w_in': dense(ks[2], (DEPTH, 2, D_MODEL, 2 * D_FF), D_MODEL),
        'ffn_w_out': dense(ks[3], (DEPTH, 2, D_FF, D_MODEL), D_FF),
        'a_w_in': dense(ks[4], (N_A_LAYERS, D_MODEL, A_IN_WIDTH), D_MODEL),
        'a_lb_logits': 0.5 * jax.random.normal(ks[5], (N_A_LAYERS + 1, A_FORGET_DIM), jnp.float32),
        'a_out_gain': gain(ks[6], (N_A_LAYERS, A_VAL_DIM)),
        'a_w_out': dense(ks[7], (N_A_LAYERS, A_VAL_WIDTH, D_MODEL), A_VAL_WIDTH),
        'kv_norm': gain(ks[8], (D_MODEL,)),
        'w_kv': dense(ks[9], (D_MODEL, 2 * B_QKV_WIDTH), D_MODEL),
        'k_gain': gain(ks[10], (B_GROUPS, B_HEAD_DIM)),
        'b_w_q': dense(ks[11], (N_B_LAYERS, D_MODEL, B_QKV_WIDTH), D_MODEL),
        'b_q_gain': gain(ks[12], (N_B_LAYERS, B_GROUPS, B_HEAD_DIM)),
        'b_w_o': dense(ks[13], (N_B_LAYERS, B_HEADS * B_HEAD_DIM, D_MODEL), B_HEADS * B_HEAD_DIM),
        'rel_bias': 0.5 * jax.random.normal(ks[14], (NUM_BUCKETS, B_GROUPS * B_HEADS), jnp.float32),
    }


def reference(x, norm_gain, ffn_w_in, ffn_w_out, a_w_in, a_lb_logits, a_out_gain, a_w_out,
              kv_norm, w_kv, k_gain, b_w_q, b_q_gain, b_w_o, rel_bias):
    lower_bounds = _lower_bounds(a_lb_logits)
    k_pads, v_pads = None, None
    for layer in range(DEPTH):
        if layer == N_A_LAYERS:
            k_pads, v_pads = _shared_kv(x, kv_norm, w_kv, k_gain)
        x = x + 0.5 * _swiglu(_rms(x, norm_gain[layer, 0]), ffn_w_in[layer, 0], ffn_w_out[layer, 0])
        h = _rms(x, norm_gain[layer, 1])
        if layer < N_A_LAYERS:
            x = x + _hgrn2(h, a_w_in[layer], lower_bounds[layer], a_out_gain[layer], a_w_out[layer])
        else:
            j = layer - N_A_LAYERS
            x = x + _dilated_attention(h, b_w_q[j], b_q_gain[j], b_w_o[j], k_pads, v_pads, rel_bias)
        x = x + 0.5 * _swiglu(_rms(x, norm_gain[layer, 2]), ffn_w_in[layer, 1], ffn_w_out[layer, 1])
    return x
```

```python
import numpy as np
import concourse.bass as bass
import concourse.mybir as mybir

F32 = mybir.dt.float32
BF16 = mybir.dt.bfloat16
AF = mybir.ActivationFunctionType
ALU = mybir.AluOpType

ENGS = ("pe", "act", "dve", "pool", "sp")
SAME_ENG_SYNC = True


class Buf:
    def __init__(self, name, t):
        self.name = name
        self.t = t
        self.writers = []
        self.readers = []
        self.sem_w = None
        self.cnt_w = 0
        self.sem_r = None
        self.cnt_r = 0

    def __getitem__(self, idx):
        return self.t[idx]


class Op:
    __slots__ = ("eng", "emit", "waits", "is_dma", "done_sem", "done_val", "needs_inc", "idx")

    def __init__(self, eng, emit, is_dma):
        self.eng = eng
        self.emit = emit
        self.is_dma = is_dma
        self.waits = []
        self.done_sem = None
        self.done_val = None
        self.needs_inc = False


class Prog:
    def __init__(self, nc):
        self.nc = nc
        self.ops = {e: [] for e in ENGS}
        self.eng_sem = {}
        self._ctx = []
        self.all_bufs = []
        self.nsem = 0
        self.dma_pending = []
        self.scopes = []

    def _sem(self, name):
        self.nsem += 1
        return self.nc.alloc_semaphore(f"s{self.nsem}_{name}")

    def begin_scope(self):
        self.scopes.append([])

    def end_scope(self):
        self.barrier()
        for g in reversed(self.scopes.pop()):
            g.__exit__(None, None, None)

    def barrier(self):
        toks = {}
        for e in ENGS:
            o = Op(e, lambda eng: eng.nop(), False)
            for dop in self.dma_pending:
                self._dep(o, dop)
            self.ops[e].append(o)
            toks[e] = o
        self.dma_pending = []
        for e in ENGS:
            o2 = Op(e, lambda eng: eng.nop(), False)
            for e2 in ENGS:
                if e2 != e:
                    self._dep(o2, toks[e2])
            self.ops[e].append(o2)

    def sbuf(self, name, shape, dt):
        name = f"{name}_{len(self.all_bufs)}"
        g = self.nc.sbuf_tensor("sb_" + name, list(shape), dt)
        t = g.__enter__()
        if self.scopes:
            self.scopes[-1].append(g)
        else:
            self._ctx.append(g)
        b = Buf(name, t)
        self.all_bufs.append(b)
        return b

    def psum(self, name, shape, dt):
        g = self.nc.psum_tensor("pp_" + name, list(shape), dt)
        t = g.__enter__()
        self._ctx.append(g)
        b = Buf(name, t)
        self.all_bufs.append(b)
        return b

    def dram(self, name, shape, dt, kind="Internal"):
        t = self.nc.dram_tensor(name, list(shape), dt, kind=kind)
        b = Buf(name, t.ap())
        self.all_bufs.append(b)
        return b

    def view(self, name, ap):
        b = Buf(name, ap)
        self.all_bufs.append(b)
        return b

    def _dep(self, op, prod):
        if prod is op:
            return
        if prod.eng == op.eng and not prod.is_dma and not op.is_dma and (not SAME_ENG_SYNC or op.eng == "pe"):
            return
        op.waits.append(prod)
        if not prod.is_dma:
            prod.needs_inc = True

    def op(self, eng, emit, reads=(), writes=()):
        o = Op(eng, emit, False)
        self._track(o, reads, writes)
        self.ops[eng].append(o)
        return o

    def dma(self, eng, emit, src, dst, sem_owner=None, reads=(), writes=()):
        o = Op(eng, emit, True)
        if sem_owner is None:
            raise ValueError("sem_owner required")
        owner, kind = sem_owner
        if kind == "w":
            if owner.sem_w is None:
                owner.sem_w = self._sem(owner.name + "_w")
            owner.cnt_w += 16
            o.done_sem, o.done_val = owner.sem_w, owner.cnt_w
        else:
            if owner.sem_r is None:
                owner.sem_r = self._sem(owner.name + "_r")
            owner.cnt_r += 16
            o.done_sem, o.done_val = owner.sem_r, owner.cnt_r
        self._track(o, [src] + list(reads), [dst] + list(writes))
        self.ops[eng].append(o)
        self.dma_pending.append(o)
        return o

    def _track(self, o, reads, writes):
        for b in reads:
            for w in b.writers:
                self._dep(o, w)
        for b in writes:
            for w in b.writers:
                self._dep(o, w)
            for r in b.readers:
                self._dep(o, r)
        for b in reads:
            if not o.is_dma:
                b.readers = [r for r in b.readers if r.is_dma or r.eng != o.eng]
            b.readers.append(o)
        for b in writes:
            b.writers = [o]
            b.readers = []

    def emit(self, final_bufs=()):
        nc = self.nc
        for e in ENGS:
            self.eng_sem[e] = self._sem("eng_" + e)
        for e in ("pe", "act", "dve", "pool", "sp"):
            c = 0
            for o in self.ops[e]:
                if o.is_dma:
                    continue
                if o.needs_inc:
                    c += 1
                    o.done_sem, o.done_val = self.eng_sem.get(e), c
        final_waits = []
        for b in final_bufs:
            for w in b.writers:
                final_waits.append((w.done_sem, w.done_val))
        engobj = {"pe": "tensor", "act": "scalar", "dve": "vector", "pool": "gpsimd", "sp": "sync"}
        stats = {}
        with nc.Block() as block:
            for e in ENGS:
                ops = self.ops[e]

                def body(eng, ops=ops, e=e):
                    waited = {}
                    nw = 0
                    for o in ops:
                        for p in o.waits:
                            key = id(p.done_sem)
                            if waited.get(key, 0) >= p.done_val:
                                continue
                            eng.wait_ge(p.done_sem, p.done_val)
                            waited[key] = p.done_val
                            nw += 1
                        inst = o.emit(eng)
                        if o.is_dma:
                            inst.then_inc(o.done_sem, 16)
                        elif o.needs_inc:
                            inst.then_inc(o.done_sem, 1)
                    if e == "sp":
                        for (s, v) in final_waits:
                            eng.wait_ge(s, v)
                    stats[e] = (len(ops), nw)

                getattr(block, engobj[e])(body)
        self.stats = stats
        return stats

    def close(self):
        for g in reversed(self._ctx):
            g.__exit__(None, None, None)

import numpy as np

D = 1024
DFF = 2816
KC = D // 128
FC = DFF // 128
TT = 512
EPS = 1e-6


class Ctx:
    def __init__(self, nc):
        self.nc = nc
        self.P = Prog(nc)
        P = self.P
        self.ps = [P.psum(f"ps{i}", [128, 512], F32) for i in range(8)]
        self.ps_i = 0
        self.rr = {}

    def psum(self):
        b = self.ps[self.ps_i % len(self.ps)]
        self.ps_i += 1
        return b

    def rot(self, key, bufs):
        i = self.rr.get(key, 0)
        self.rr[key] = i + 1
        return bufs[i % len(bufs)]


def load_consts(C, ident_d, identb_d, ones_d, onesbd_d=None):
    P = C.P
    if onesbd_d is not None:
        C.onesbd = P.sbuf("onesbd", [128, 128], BF16)
        P.dma("sp", lambda e: e.dma_start(out=C.onesbd[:], in_=onesbd_d[:]), onesbd_d, C.onesbd, sem_owner=(C.onesbd, "w"))
    C.ident = P.sbuf("ident", [128, 128], F32)
    C.identb = P.sbuf("identb", [128, 128], BF16)
    C.ones = P.sbuf("onesb", [128, 128], BF16)
    C.epsc = P.sbuf("epsc", [128, 1], F32)
    P.op("dve", lambda e: e.memset(C.epsc[:], EPS), writes=[C.epsc])
    for sb, d in ((C.ident, ident_d), (C.identb, identb_d), (C.ones, ones_d)):
        P.dma("sp", lambda e, sb=sb, d=d: e.dma_start(out=sb[:], in_=d[:]), d, sb, sem_owner=(sb, "w"))


def cast_weight(C, w_d, wb_d, K, N, stage, cast_engs=("pool", "act", "dve")):
    P = C.P
    CW = 2048
    for kc in range(K // 128):
        for c0 in range(0, N, CW):
            cw = min(CW, N - c0)
            st32, st16 = C.rot("caststage", stage)
            P.dma("sp", lambda e, st32=st32, kc=kc, c0=c0, cw=cw: e.dma_start(
                out=st32[:, 0:cw], in_=w_d[kc * 128:(kc + 1) * 128, c0:c0 + cw]),
                w_d, st32, sem_owner=(st32, "w"))
            ce = C.rot("casteng", cast_engs)
            if ce == "act":
                P.op("act", lambda e, st32=st32, st16=st16, cw=cw: e.copy(out=st16[:, 0:cw], in_=st32[:, 0:cw]),
                     reads=[st32], writes=[st16])
            else:
                P.op(ce, lambda e, st32=st32, st16=st16, cw=cw: e.tensor_copy(out=st16[:, 0:cw], in_=st32[:, 0:cw]),
                     reads=[st32], writes=[st16])
            P.dma("pool", lambda e, st16=st16, kc=kc, c0=c0, cw=cw: e.dma_start(
                out=wb_d[kc * 128:(kc + 1) * 128, c0:c0 + cw], in_=st16[:, 0:cw]),
                st16, wb_d, sem_owner=(st16, "r"))


def load_xT(C, x_d, t0, xT, xin):
    P = C.P
    for s in range(TT // 128):
        xi = C.rot("xin", xin)
        P.dma("sp", lambda e, xi=xi, s=s: e.dma_start(out=xi[:], in_=x_d[t0 + s * 128:t0 + (s + 1) * 128, :]),
              x_d, xi, sem_owner=(xi, "w"))
        for g in range(2):
            ps = C.psum()
            for kk in range(4):
                k = g * 4 + kk
                P.op("pe", lambda e, ps=ps, xi=xi, k=k, kk=kk: e.transpose(
                    out=ps[:, kk * 128:(kk + 1) * 128], in_=xi[:, k * 128:(k + 1) * 128], identity=C.ident[:]),
                    reads=[xi, C.ident], writes=[ps])
            eng = "dve" if g == 0 else "act"
            if eng == "dve":
                P.op("dve", lambda e, ps=ps, g=g, s=s: e.tensor_copy(
                    out=xT[:, g * 4:(g + 1) * 4, s * 128:(s + 1) * 128],
                    in_=ps[:].rearrange("p (k t) -> p k t", k=4)), reads=[ps], writes=[xT])
            else:
                P.op("act", lambda e, ps=ps, g=g, s=s: e.copy(
                    out=xT[:, g * 4:(g + 1) * 4, s * 128:(s + 1) * 128],
                    in_=ps[:].rearrange("p (k t) -> p k t", k=4)), reads=[ps], writes=[xT])


def store_xT_tokmajor(C, xT, y_d, t0, xout):
    P = C.P
    for s in range(TT // 128):
        xo = C.rot("xout", xout)
        for g in range(2):
            ps = C.psum()
            for kk in range(4):
                k = g * 4 + kk
                P.op("pe", lambda e, ps=ps, k=k, kk=kk, s=s: e.transpose(
                    out=ps[:, kk * 128:(kk + 1) * 128], in_=xT[:, k, s * 128:(s + 1) * 128], identity=C.ident[:]),
                    reads=[xT, C.ident], writes=[ps])
            if g == 0:
                P.op("dve", lambda e, ps=ps, xo=xo: e.tensor_copy(out=xo[:, 0:512], in_=ps[:]), reads=[ps], writes=[xo])
            else:
                P.op("act", lambda e, ps=ps, xo=xo: e.copy(out=xo[:, 512:1024], in_=ps[:]), reads=[ps], writes=[xo])
        P.dma("pool", lambda e, xo=xo, s=s: e.dma_start(out=y_d[t0 + s * 128:t0 + (s + 1) * 128, :], in_=xo[:]),
              xo, y_d, sem_owner=(xo, "r"))


def rms_norm_T(C, xT, gcol, hT, sq, rstd, extra_w=()):
    P = C.P
    gbuf, g0 = gcol
    ps = C.psum()
    for k in range(KC):
        sqk = C.rot("sq", sq)
        eng = "act" if k % 2 == 0 else "dve"
        if eng == "act":
            P.op("act", lambda e, sqk=sqk, k=k: e.activation(out=sqk[:], in_=xT[:, k, :], func=AF.Square),
                 reads=[xT], writes=[sqk])
        else:
            P.op("dve", lambda e, sqk=sqk, k=k: e.tensor_tensor(out=sqk[:], in0=xT[:, k, :], in1=xT[:, k, :], op=ALU.mult),
                 reads=[xT], writes=[sqk])
        P.op("pe", lambda e, ps=ps, sqk=sqk, k=k: e.matmul(ps[:], C.ones[:], sqk[:], start=(k == 0), stop=(k == KC - 1)),
             reads=[sqk, C.ones], writes=[ps])
    P.op("act", lambda e, ps=ps: e.activation(out=rstd[:], in_=ps[:], func=AF.Ln, bias=C.epsc[:, 0:1], scale=1.0 / D),
         reads=[ps, C.epsc], writes=[rstd])
    P.op("act", lambda e: e.activation(out=rstd[:], in_=rstd[:], func=AF.Exp, scale=-0.5), reads=[rstd], writes=[rstd])
    for k in range(KC):
        P.op("dve", lambda e, k=k: e.scalar_tensor_tensor(
            out=hT[:, k, :], in0=xT[:, k, :], scalar=gbuf[:, g0 + k:g0 + k + 1], in1=rstd[:],
            op0=ALU.mult, op1=ALU.mult), reads=[xT, gbuf, rstd], writes=[hT] + list(extra_w))


def ffn_tile(C, xT, gcol, win_d, wout_d, B):
    P = C.P
    hT, aT = B["hT"], B["aT"]
    rms_norm_T(C, xT, gcol, hT, B["sq"], B["rstd"])
    GJ = 2
    for j0 in range(0, FC, GJ):
        w = C.rot("win", B["win"])
        for half in range(2):
            c0 = half * DFF + j0 * 128
            P.dma("sp", lambda e, w=w, half=half, c0=c0: e.dma_start(
                out=w[:, :, half, :], in_=win_d[:, c0:c0 + GJ * 128].rearrange("(k p) c -> p k c", p=128)),
                win_d, w, sem_owner=(w, "w"))
        for jj in range(GJ):
            j = j0 + jj
            psg = C.psum()
            psu = C.psum()
            for k in range(KC):
                P.op("pe", lambda e, psg=psg, w=w, k=k, jj=jj: e.matmul(
                    psg[:], w[:, k, 0, jj * 128:(jj + 1) * 128], hT[:, k, :], start=(k == 0), stop=(k == KC - 1)),
                    reads=[w, hT], writes=[psg])
            for k in range(KC):
                P.op("pe", lambda e, psu=psu, w=w, k=k, jj=jj: e.matmul(
                    psu[:], w[:, k, 1, jj * 128:(jj + 1) * 128], hT[:, k, :], start=(k == 0), stop=(k == KC - 1)),
                    reads=[w, hT], writes=[psu])
            sg = C.rot("sg", B["sg"])
            P.op("act", lambda e, psg=psg, sg=sg: e.activation(out=sg[:], in_=psg[:], func=AF.Silu),
                 reads=[psg], writes=[sg])
            P.op("dve", lambda e, psu=psu, sg=sg, j=j: e.tensor_tensor(out=aT[:, j, :], in0=psu[:], in1=sg[:], op=ALU.mult),
                 reads=[psu, sg], writes=[aT])
    MG = 2
    for m0 in range(0, KC, MG):
        w = C.rot("wout", B["wout"])
        P.dma("sp", lambda e, w=w, m0=m0: e.dma_start(
            out=w[:], in_=wout_d[:, m0 * 128:(m0 + MG) * 128].rearrange("(j p) c -> p j c", p=128)),
            wout_d, w, sem_owner=(w, "w"))
        for mm in range(MG):
            m = m0 + mm
            ps = C.psum()
            for j in range(FC):
                P.op("pe", lambda e, ps=ps, w=w, j=j, mm=mm: e.matmul(
                    ps[:], w[:, j, mm * 128:(mm + 1) * 128], aT[:, j, :], start=(j == 0), stop=(j == FC - 1)),
                    reads=[w, aT], writes=[ps])
            P.op("dve", lambda e, ps=ps, m=m: e.scalar_tensor_tensor(
                out=xT[:, m, :], in0=ps[:], scalar=0.5, in1=xT[:, m, :], op0=ALU.mult, op1=ALU.add),
                reads=[ps, xT], writes=[xT])


def ffn_bufs(C):
    P = C.P
    B = {}
    B["hT"] = P.sbuf("hT", [128, KC, TT], BF16)
    B["aT"] = P.sbuf("aT", [128, FC, TT], BF16)
    B["sq"] = [P.sbuf(f"sq{i}", [128, TT], BF16) for i in range(2)]
    B["rstd"] = P.sbuf("rstd", [128, TT], F32)
    B["win"] = [P.sbuf(f"win{i}", [128, KC, 2, 256], BF16) for i in range(3)]
    B["wout"] = [P.sbuf(f"wout{i}", [128, FC, 256], BF16) for i in range(2)]
    B["sg"] = [P.sbuf(f"sg{i}", [128, TT], F32) for i in range(2)]
    return B


def load_resid(C, xa_tile, xT):
    C.P.dma("sp", lambda e: e.dma_start(out=xT[:], in_=xa_tile[:].rearrange("(k p) t -> p k t", p=128)),
            xa_tile, xT, sem_owner=(xT, "w"))


def store_resid(C, xT, xa_tile):
    C.P.dma("pool", lambda e: e.dma_start(out=xa_tile[:].rearrange("(k p) t -> p k t", p=128), in_=xT[:]),
            xT, xa_tile, sem_owner=(xT, "r"))


def rstd_from_ps(C, ps_ap, ps_buf, rstd, n, inv_n):
    P = C.P
    P.op("act", lambda e: e.activation(out=rstd[:, 0:n], in_=ps_ap, func=AF.Ln, bias=C.epsc[:, 0:1], scale=inv_n),
         reads=[ps_buf, C.epsc], writes=[rstd])
    P.op("act", lambda e: e.activation(out=rstd[:, 0:n], in_=rstd[:, 0:n], func=AF.Exp, scale=-0.5),
         reads=[rstd], writes=[rstd])


NH = 8


def hgrn_consts(C, lbl_d, mask2_d):
    P = C.P
    lbl = P.sbuf("lbl", [128, 2, NH], F32)
    P.dma("sp", lambda e: e.dma_start(out=lbl[:], in_=lbl_d[:]), lbl_d, lbl, sem_owner=(lbl, "w"))
    C.lb = P.sbuf("lb", [128, NH], F32)
    C.oml = P.sbuf("oml", [128, NH], F32)
    C.noml = P.sbuf("noml", [128, NH], F32)
    dif = P.sbuf("lbdif", [128, NH], F32)
    P.op("dve", lambda e: e.tensor_tensor(out=dif[:], in0=lbl[:, 0, :], in1=lbl[:, 1, :], op=ALU.subtract),
         reads=[lbl], writes=[dif])
    P.op("act", lambda e: e.activation(out=C.lb[:], in_=dif[:], func=AF.Sigmoid), reads=[dif], writes=[C.lb])
    P.op("act", lambda e: e.activation(out=C.oml[:], in_=dif[:], func=AF.Sigmoid, scale=-1.0), reads=[dif], writes=[C.oml])
    P.op("dve", lambda e: e.tensor_scalar(out=C.noml[:], in0=C.oml[:], scalar1=-1.0, scalar2=None, op0=ALU.mult),
         reads=[C.oml], writes=[C.noml])
    C.mask2 = P.sbuf("mask2", [128, 128], F32)
    P.dma("sp", lambda e: e.dma_start(out=C.mask2[:], in_=mask2_d[:]), mask2_d, C.mask2, sem_owner=(C.mask2, "w"))
    C.notstart = P.sbuf("notstart", [128, TT], F32)
    P.op("dve", lambda e: e.memset(C.notstart[:], 1.0), writes=[C.notstart])
    P.op("dve", lambda e: e.memset(C.notstart[:, 0::64], 0.0), writes=[C.notstart])


def hgrn_bufs(C, full):
    P = C.P
    B = {}
    B["wa"] = [P.sbuf(f"wa{i}", [128, KC, 4, 128], BF16) for i in range(2)]
    for nm in ("sig", "glog", "kt", "cum", "dlt", "en"):
        B[nm] = [P.sbuf(f"hg_{nm}{i}", [128, TT], F32) for i in range(2)]
    B["khT"] = [P.sbuf(f"khT{i}", [128, TT], BF16) for i in range(2)]
    B["V"] = [P.sbuf(f"hV{i}", [128, 4, 128], BF16) for i in range(2)]
    B["KH"] = [P.sbuf(f"hKH{i}", [128, 4, 128], BF16) for i in range(2)]
    B["sc"] = [P.sbuf(f"hsc{i}", [128, 4, 8], F32) for i in range(2)]
    B["S"] = [P.sbuf(f"hS{h}", [128, 128], F32) for h in range(NH)]
    B["dlog"] = P.sbuf("dlog", [128, NH], F32)
    B["dtmp"] = P.sbuf("dtmp", [128, 1], F32)
    if full:
        for nm in ("qs", "ep", "gs", "oT", "og"):
            B[nm] = [P.sbuf(f"hg_{nm}{i}", [128, TT], F32) for i in range(2)]
        B["qhT"] = [P.sbuf(f"qhT{i}", [128, TT], BF16) for i in range(2)]
        B["AT"] = [P.sbuf(f"hAT{i}", [128, 128], BF16) for i in range(2)]
        B["Sp"] = [P.sbuf(f"hSp{i}", [128, 128], BF16) for i in range(3)]
        B["ohT"] = P.sbuf("ohT", [128, NH, TT], BF16)
        B["osq"] = [P.sbuf(f"osq{i}", [128, TT], BF16) for i in range(2)]
        B["orstd"] = [P.sbuf(f"orstd{i}", [128, TT], F32) for i in range(2)]
        B["wo"] = P.sbuf("wao", [128, NH, D], BF16)
    return B


def hgrn_tile(C, hT, xT, wa_d, B, full, og_col=None):
    P = C.P
    for h in range(NH):
        w = C.rot("wa", B["wa"])
        parts = (0, 1, 2, 3) if full else (1, 2)
        for part in parts:
            c0 = part * 1024 + h * 128
            P.dma("sp", lambda e, w=w, part=part, c0=c0: e.dma_start(
                out=w[:, :, part, :], in_=wa_d[:, c0:c0 + 128].rearrange("(k p) c -> p k c", p=128)),
                wa_d, w, sem_owner=(w, "w"))

        def proj(part, w=w):
            ps = C.psum()
            for k in range(KC):
                P.op("pe", lambda e, ps=ps, k=k, w=w, part=part: e.matmul(ps[:], w[:, k, part, :], hT[:, k, :],
                                                          start=(k == 0), stop=(k == KC - 1)),
                     reads=[w, hT], writes=[ps])
            return ps

        sig = C.rot("sig", B["sig"]); glog = C.rot("glog", B["glog"]); kt = C.rot("kt", B["kt"])
        cum = C.rot("cum", B["cum"]); dlt = C.rot("dlt", B["dlt"]); en = C.rot("en", B["en"])
        khT = C.rot("khT", B["khT"]); V = C.rot("V", B["V"]); KH = C.rot("KH", B["KH"]); sc = C.rot("sc", B["sc"])
        S = B["S"][h]
        ps_f = proj(1)
        P.op("act", lambda e, ps_f=ps_f, sig=sig: e.activation(out=sig[:], in_=ps_f[:], func=AF.Sigmoid),
             reads=[ps_f], writes=[sig])
        P.op("act", lambda e, sig=sig, glog=glog, h=h: e.activation(
            out=glog[:], in_=sig[:], func=AF.Ln, bias=C.lb[:, h:h + 1], scale=C.oml[:, h:h + 1]),
            reads=[sig, C.lb, C.oml], writes=[glog])
        P.op("dve", lambda e, sig=sig, kt=kt, h=h: e.tensor_scalar(
            out=kt[:], in0=sig[:], scalar1=C.noml[:, h:h + 1], scalar2=C.oml[:, h:h + 1], op0=ALU.mult, op1=ALU.add),
            reads=[sig, C.noml, C.oml], writes=[kt])
        P.op("dve", lambda e, cum=cum, glog=glog: e.tensor_tensor_scan(
            out=cum[:], data0=C.notstart[:], data1=glog[:], initial=0.0, op0=ALU.mult, op1=ALU.add),
            reads=[C.notstart, glog], writes=[cum])
        P.op("dve", lambda e, cum=cum, dlt=dlt: e.tensor_tensor(
            out=dlt[:].rearrange("p (c t) -> p c t", t=64), in0=cum[:].rearrange("p (c t) -> p c t", t=64),
            in1=cum[:, 31::64].unsqueeze(2).broadcast_to([128, 8, 64]), op=ALU.subtract),
            reads=[cum], writes=[dlt])
        P.op("act", lambda e, dlt=dlt, en=en: e.activation(out=en[:], in_=dlt[:], func=AF.Exp, scale=-1.0),
             reads=[dlt], writes=[en])
        P.op("dve", lambda e, kt=kt, en=en, khT=khT: e.tensor_tensor(out=khT[:], in0=kt[:], in1=en[:], op=ALU.mult),
             reads=[kt, en], writes=[khT])
        P.op("act", lambda e, cum=cum, sc=sc: e.activation(out=sc[:, 0, :], in_=cum[:, 31::64], func=AF.Exp),
             reads=[cum], writes=[sc])
        P.op("act", lambda e, cum=cum, sc=sc: e.activation(out=sc[:, 1, :], in_=cum[:, 63::64], func=AF.Exp),
             reads=[cum], writes=[sc])
        P.op("dve", lambda e, cum=cum, sc=sc: e.tensor_tensor(out=sc[:, 2, :], in0=cum[:, 63::64], in1=cum[:, 31::64],
                                                              op=ALU.subtract), reads=[cum], writes=[sc])
        P.op("act", lambda e, sc=sc: e.activation(out=sc[:, 3, :], in_=sc[:, 2, :], func=AF.Exp), reads=[sc], writes=[sc])
        if not full:
            P.op("dve", lambda e, cum=cum: e.tensor_reduce(out=B["dtmp"][:], in_=cum[:, 63::64], axis=mybir.AxisListType.X,
                                                           op=ALU.add), reads=[cum], writes=[B["dtmp"]])
            P.op("dve", lambda e, h=h: e.tensor_tensor(out=B["dlog"][:, h:h + 1], in0=B["dlog"][:, h:h + 1],
                                                      in1=B["dtmp"][:], op=ALU.add),
                 reads=[B["dlog"], B["dtmp"]], writes=[B["dlog"]])
        dbg(C, "sig", sig, [128, TT], F32); dbg(C, "glog", glog, [128, TT], F32); dbg(C, "kt", kt, [128, TT], F32)
        dbg(C, "cum", cum, [128, TT], F32); dbg(C, "dlt", dlt, [128, TT], F32); dbg(C, "khT", khT, [128, TT], BF16)
        dbg(C, "sc", sc, [128, 4, 8], F32)
        psv = C.psum()
        for blk in range(4):
            for k in range(KC):
                P.op("pe", lambda e, psv=psv, blk=blk, k=k, w=w: e.matmul(
                    psv[:, blk * 128:(blk + 1) * 128], hT[:, k, blk * 128:(blk + 1) * 128], w[:, k, 2, :],
                    start=(k == 0), stop=(k == KC - 1)), reads=[hT, w], writes=[psv])
        P.op("act", lambda e, psv=psv, V=V: e.copy(out=V[:].rearrange("p b e -> p (b e)"), in_=psv[:]),
             reads=[psv], writes=[V])
        pskh = C.psum()
        for blk in range(4):
            P.op("pe", lambda e, pskh=pskh, blk=blk, khT=khT: e.matmul(
                pskh[:, blk * 128:(blk + 1) * 128], khT[:, blk * 128:(blk + 1) * 128], C.identb[:], start=True, stop=True),
                reads=[khT, C.identb], writes=[pskh])
        P.op("dve", lambda e, pskh=pskh, KH=KH: e.tensor_copy(out=KH[:].rearrange("p b e -> p (b e)"), in_=pskh[:]),
             reads=[pskh], writes=[KH])
        if full:
            qs = C.rot("qs", B["qs"]); ep = C.rot("ep", B["ep"]); gs = C.rot("gs", B["gs"])
            oT = C.rot("oT", B["oT"]); qhT = C.rot("qhT", B["qhT"])
            ps_q = proj(0)
            P.op("act", lambda e, ps_q=ps_q, qs=qs: e.activation(out=qs[:], in_=ps_q[:], func=AF.Silu),
                 reads=[ps_q], writes=[qs])
            P.op("act", lambda e, dlt=dlt, ep=ep: e.activation(out=ep[:], in_=dlt[:], func=AF.Exp), reads=[dlt], writes=[ep])
            P.op("dve", lambda e, qs=qs, ep=ep, qhT=qhT: e.tensor_tensor(out=qhT[:], in0=qs[:], in1=ep[:], op=ALU.mult),
                 reads=[qs, ep], writes=[qhT])
            ps_g = proj(3)
            P.op("act", lambda e, ps_g=ps_g, gs=gs: e.activation(out=gs[:], in_=ps_g[:], func=AF.Silu),
                 reads=[ps_g], writes=[gs])
        for blk in range(4):
            if full:
                pss = C.psum()
                P.op("pe", lambda e, pss=pss, blk=blk, khT=khT, qhT=qhT: e.matmul(
                    pss[:, 0:128], khT[:, blk * 128:(blk + 1) * 128], qhT[:, blk * 128:(blk + 1) * 128], start=True, stop=True),
                    reads=[khT, qhT], writes=[pss])
                AT = C.rot("AT", B["AT"])
                P.op("dve", lambda e, pss=pss, AT=AT: e.tensor_tensor(out=AT[:], in0=pss[:, 0:128], in1=C.mask2[:], op=ALU.mult),
                     reads=[pss, C.mask2], writes=[AT])
                pso = C.psum()
                P.op("pe", lambda e, pso=pso, blk=blk, V=V, AT=AT: e.matmul(
                    pso[:, 0:128], V[:, blk, :], AT[:], start=True, stop=False), reads=[V, AT], writes=[pso])
            for c in range(2):
                ch = blk * 2 + c
                if full:
                    Sp = C.rot("Sp", B["Sp"])
                    P.op("pool", lambda e, Sp=Sp, S=S, sc=sc, ch=ch: e.tensor_scalar(
                        out=Sp[:], in0=S[:], scalar1=sc[:, 0, ch:ch + 1], scalar2=None, op0=ALU.mult),
                        reads=[S, sc], writes=[Sp])
                    P.op("pe", lambda e, pso=pso, Sp=Sp, qhT=qhT, ch=ch, c=c: e.matmul(
                        pso[:, c * 64:(c + 1) * 64], Sp[:], qhT[:, ch * 64:(ch + 1) * 64], start=False, stop=(c == 1)),
                        reads=[Sp, qhT], writes=[pso])
                psu = C.psum()
                P.op("pe", lambda e, psu=psu, KH=KH, V=V, blk=blk, c=c: e.matmul(
                    psu[:, 0:128], KH[c * 64:(c + 1) * 64, blk, :], V[c * 64:(c + 1) * 64, blk, :], start=True, stop=True),
                    reads=[KH, V], writes=[psu])
                P.op("dve", lambda e, S=S, sc=sc, ch=ch: e.tensor_scalar(
                    out=S[:], in0=S[:], scalar1=sc[:, 1, ch:ch + 1], scalar2=None, op0=ALU.mult),
                    reads=[S, sc], writes=[S])
                P.op("dve", lambda e, S=S, psu=psu, sc=sc, ch=ch: e.scalar_tensor_tensor(
                    out=S[:], in0=psu[:, 0:128], scalar=sc[:, 3, ch:ch + 1], in1=S[:], op0=ALU.mult, op1=ALU.add),
                    reads=[psu, sc, S], writes=[S])
            if full:
                P.op("act", lambda e, pso=pso, oT=oT, blk=blk: e.copy(out=oT[:, blk * 128:(blk + 1) * 128], in_=pso[:, 0:128]),
                     reads=[pso], writes=[oT])
        if full:
            dbg(C, "V", V, [128, 4, 128], BF16); dbg(C, "KH", KH, [128, 4, 128], BF16); dbg(C, "qhT", qhT, [128, TT], BF16)
            dbg(C, "oT", oT, [128, TT], F32); dbg(C, "S0", S, [128, 128], F32); dbg(C, "gs", gs, [128, TT], F32)
            osq = C.rot("osq", B["osq"]); orstd = C.rot("orstd", B["orstd"]); og = C.rot("og", B["og"])
            P.op("act", lambda e, oT=oT, osq=osq: e.activation(out=osq[:], in_=oT[:], func=AF.Square), reads=[oT], writes=[osq])
            pss = C.psum()
            P.op("pe", lambda e, pss=pss, osq=osq: e.matmul(pss[:], C.ones[:], osq[:], start=True, stop=True),
                 reads=[C.ones, osq], writes=[pss])
            rstd_from_ps(C, pss[:], pss, orstd, TT, 1.0 / 128)
            gbuf, g0 = og_col
            P.op("dve", lambda e, oT=oT, orstd=orstd, og=og: e.scalar_tensor_tensor(
                out=og[:], in0=oT[:], scalar=gbuf[:, g0:g0 + 1], in1=orstd[:], op0=ALU.mult, op1=ALU.mult),
                reads=[oT, gbuf, orstd], writes=[og])
            P.op("dve", lambda e, og=og, gs=gs, h=h: e.tensor_tensor(out=B["ohT"][:, h, :], in0=og[:], in1=gs[:], op=ALU.mult),
                 reads=[og, gs], writes=[B["ohT"]])
    if full:
        wo = B["wo"]
        for m in range(KC):
            ps = C.psum()
            for h in range(NH):
                P.op("pe", lambda e, ps=ps, h=h, m=m: e.matmul(ps[:], wo[:, h, m * 128:(m + 1) * 128], B["ohT"][:, h, :],
                                                              start=(h == 0), stop=(h == NH - 1)),
                     reads=[wo, B["ohT"]], writes=[ps])
            P.op("dve", lambda e, ps=ps, m=m: e.tensor_tensor(out=xT[:, m, :], in0=ps[:], in1=xT[:, m, :], op=ALU.add),
                 reads=[ps, xT], writes=[xT])


DBG = {}


def dbg(C, name, buf, shape, dt):
    if not DBG.get("on") or name in DBG:
        return
    P = C.P
    d = P.dram("dbg_" + name, list(shape), dt, kind="ExternalOutput")
    DBG[name] = d
    P.dma("pool", lambda e: e.dma_start(out=d[:], in_=buf[:]), buf, d, sem_owner=(buf, "r"))


NG = 3
HALO = 2048
DILS = (1, 4, 16)


def kv_bufs(C):
    P = C.P
    B = {}
    B["wk"] = [P.sbuf(f"wk{i}", [128, KC, 256], BF16) for i in range(2)]
    B["wv"] = [P.sbuf(f"wv{i}", [128, KC, 512], BF16) for i in range(2)]
    B["ksq"] = [P.sbuf(f"ksq{i}", [128, TT], BF16) for i in range(2)]
    B["krstd"] = [P.sbuf(f"krstd{i}", [128, TT], F32) for i in range(2)]
    B["kn"] = [P.sbuf(f"kn{i}", [128, TT], BF16) for i in range(3)]
    B["vst"] = [P.sbuf(f"vst{i}", [128, 3072], BF16) for i in range(4)]
    return B


def kv_tile(C, hT, wkv_d, kT_d, vtok_d, t0, B, kg, kv_off=0):
    P = C.P
    kgb, kg0 = kg
    for g in range(NG):
        for c2 in range(0, KC, 2):
            w = C.rot("wk", B["wk"])
            c0 = g * 1024 + c2 * 128
            P.dma("sp", lambda e, w=w, c0=c0: e.dma_start(
                out=w[:], in_=wkv_d[:, c0:c0 + 256].rearrange("(k p) c -> p k c", p=128)), wkv_d, w, sem_owner=(w, "w"))
            for cc in range(2):
                c = c2 + cc
                ps = C.psum()
                for k in range(KC):
                    P.op("pe", lambda e, ps=ps, w=w, k=k, cc=cc: e.matmul(
                        ps[:], w[:, k, cc * 128:(cc + 1) * 128], hT[:, k, :], start=(k == 0), stop=(k == KC - 1)),
                        reads=[w, hT], writes=[ps])
                sq = C.rot("ksq", B["ksq"]); rstd = C.rot("krstd", B["krstd"]); kn = C.rot("kn", B["kn"])
                P.op("act", lambda e, ps=ps, sq=sq: e.activation(out=sq[:], in_=ps[:], func=AF.Square), reads=[ps], writes=[sq])
                ps2 = C.psum()
                P.op("pe", lambda e, ps2=ps2, sq=sq: e.matmul(ps2[:], C.onesbd[:], sq[:], start=True, stop=True),
                     reads=[C.onesbd, sq], writes=[ps2])
                rstd_from_ps(C, ps2[:], ps2, rstd, TT, 1.0 / 64)
                P.op("dve", lambda e, ps=ps, rstd=rstd, kn=kn, g=g: e.scalar_tensor_tensor(
                    out=kn[:], in0=ps[:], scalar=kgb[:, kg0 + g:kg0 + g + 1], in1=rstd[:], op0=ALU.mult, op1=ALU.mult),
                    reads=[ps, kgb, rstd], writes=[kn])
                P.dma("pool", lambda e, kn=kn, g=g, c=c: e.dma_start(
                    out=kT_d[g][c * 128:(c + 1) * 128, kv_off + t0:kv_off + t0 + TT], in_=kn[:]),
                    kn, kT_d[g], sem_owner=(kn, "r"))
    vst = B["vst"]
    for j in range(6):
        w = C.rot("wv", B["wv"])
        c0 = 3072 + j * 512
        P.dma("sp", lambda e, w=w, c0=c0: e.dma_start(
            out=w[:], in_=wkv_d[:, c0:c0 + 512].rearrange("(k p) c -> p k c", p=128)), wkv_d, w, sem_owner=(w, "w"))
        for blk in range(4):
            ps = C.psum()
            for k in range(KC):
                P.op("pe", lambda e, ps=ps, w=w, k=k, blk=blk: e.matmul(
                    ps[:], hT[:, k, blk * 128:(blk + 1) * 128], w[:, k, :], start=(k == 0), stop=(k == KC - 1)),
                    reads=[w, hT], writes=[ps])
            if (j + blk) % 2 == 0:
                P.op("act", lambda e, ps=ps, blk=blk, j=j: e.copy(out=vst[blk][:, j * 512:(j + 1) * 512], in_=ps[:]),
                     reads=[ps], writes=[vst[blk]])
            else:
                P.op("dve", lambda e, ps=ps, blk=blk, j=j: e.tensor_copy(out=vst[blk][:, j * 512:(j + 1) * 512], in_=ps[:]),
                     reads=[ps], writes=[vst[blk]])
    for blk in range(4):
        r0 = kv_off + t0 + blk * 128
        P.dma("pool", lambda e, blk=blk, r0=r0: e.dma_start(out=vtok_d[r0:r0 + 128, :], in_=vst[blk][:]),
              vst[blk], vtok_d, sem_owner=(vst[blk], "r"))


ST = 2048
ATT_LIMIT = {"pairs": 8, "blocks": 16, "skipV": False, "stage": 9, "groups": 3}


def attn_consts(C, bias_d, hv_d):
    P = C.P
    C.expB = P.sbuf("expB", [128, 48, 256], BF16)
    stg = [P.sbuf(f"bstg{i}", [128, 4, 256], F32) for i in range(2)]
    for i in range(12):
        s = stg[i % 2]
        P.dma("sp", lambda e, s=s, i=i: e.dma_start(out=s[:], in_=bias_d[:, i * 4:(i + 1) * 4, :]), bias_d, s, sem_owner=(s, "w"))
        P.op("act", lambda e, s=s, i=i: e.activation(out=C.expB[:, i * 4:(i + 1) * 4, :], in_=s[:], func=AF.Exp),
             reads=[s], writes=[C.expB])
    C.hv = P.sbuf("hv", [128, 1], F32)
    P.dma("sp", lambda e: e.dma_start(out=C.hv[:], in_=hv_d[:]), hv_d, C.hv, sem_owner=(C.hv, "w"))


def attn_bufs(C):
    P = C.P
    B = {}
    B["hT"] = P.sbuf("ahT", [128, KC, ST], BF16)
    B["oT"] = P.sbuf("aoT", [128, KC, ST], BF16)
    B["KT"] = [P.sbuf(f"aKT{g}", [128, 128 * DILS[g] + ST], BF16) for g in range(NG)]
    B["Vb"] = [P.sbuf(f"aVb{g}", [128, (ST // (128 * DILS[g]) + 1) * DILS[g], 128], BF16) for g in range(NG)]
    B["qT"] = [P.sbuf(f"aqT{g}", [128, ST], BF16) for g in range(NG)]
    B["wq"] = [P.sbuf(f"awq{i}", [128, KC, 128], BF16) for i in range(2)]
    B["qsq"] = [P.sbuf(f"aqsq{i}", [128, TT], BF16) for i in range(2)]
    B["qrstd"] = [P.sbuf(f"aqrstd{i}", [128, TT], F32) for i in range(2)]
    B["expS"] = [P.sbuf(f"aexpS{i}", [128, 256], F32) for i in range(2)]
    B["PT"] = [P.sbuf(f"aPT{i}", [128, 256], BF16) for i in range(2)]
    B["NZ"] = P.sbuf("aNZ", [128, 2, ST], F32)
    B["wo"] = [P.sbuf(f"awo{i}", [128, KC, 256], BF16) for i in range(2)]
    return B


def attn_supertile(C, sidx, wq_d, kT_d, vtok_d, B, qg):
    P = C.P
    qgb, qg0 = qg
    hT = B["hT"]
    S0 = sidx * ST
    for c in range(ATT_LIMIT["pairs"]):
        for g in range(NG):
            d = DILS[g]
            span = 128 * d
            KT = B["KT"][g]
            P.dma("sp", lambda e, KT=KT, g=g, c=c, span=span: e.dma_start(
                out=KT[:], in_=kT_d[g][c * 128:(c + 1) * 128, HALO + S0 - span:HALO + S0 + ST]),
                kT_d[g], KT, sem_owner=(KT, "w"))
            Vb = B["Vb"][g]
            nu = ST // span + 1
            r0 = HALO + S0 - span
            fc0 = g * 1024 + c * 128
            for u in range(nu):
                for rr0 in range(0, d, 8):
                    rn = min(8, d - rr0)
                    src = vtok_d[r0 + u * span:r0 + (u + 1) * span, fc0:fc0 + 128].rearrange("(j r) f -> j r f", r=d)
                    P.dma("sp", lambda e, Vb=Vb, u=u, rr0=rr0, rn=rn, src=src, d=d: e.dma_start(
                        out=Vb[:, u * d + rr0:u * d + rr0 + rn, :], in_=src[:, rr0:rr0 + rn, :]),
                        vtok_d, Vb, sem_owner=(Vb, "w"))
        for g in range(NG if ATT_LIMIT["stage"] >= 1 else 0):
            w = C.rot("wq", B["wq"])
            c0 = g * 1024 + c * 128
            P.dma("sp", lambda e, w=w, c0=c0: e.dma_start(
                out=w[:], in_=wq_d[:, c0:c0 + 128].rearrange("(k p) c -> p k c", p=128)), wq_d, w, sem_owner=(w, "w"))
            qT = B["qT"][g]
            for u in range(ST // TT):
                ps = C.psum()
                for k in range(KC):
                    P.op("pe", lambda e, ps=ps, w=w, k=k, u=u: e.matmul(
                        ps[:], w[:, k, :], hT[:, k, u * TT:(u + 1) * TT], start=(k == 0), stop=(k == KC - 1)),
                        reads=[w, hT], writes=[ps])
                sq = C.rot("qsq", B["qsq"]); rstd = C.rot("qrstd", B["qrstd"])
                P.op("act", lambda e, ps=ps, sq=sq: e.activation(out=sq[:], in_=ps[:], func=AF.Square), reads=[ps], writes=[sq])
                ps2 = C.psum()
                P.op("pe", lambda e, ps2=ps2, sq=sq: e.matmul(ps2[:], C.onesbd[:], sq[:], start=True, stop=True),
                     reads=[C.onesbd, sq], writes=[ps2])
                rstd_from_ps(C, ps2[:], ps2, rstd, TT, 1.0 / 64)
                P.op("dve", lambda e, ps=ps, rstd=rstd, qT=qT, g=g, u=u: e.scalar_tensor_tensor(
                    out=qT[:, u * TT:(u + 1) * TT], in0=ps[:], scalar=qgb[:, qg0 + g:qg0 + g + 1], in1=rstd[:],
                    op0=ALU.mult, op1=ALU.mult), reads=[ps, qgb, rstd], writes=[qT])
        NZ = B["NZ"]
        for g in range(min(NG, ATT_LIMIT["groups"]) if ATT_LIMIT["stage"] >= 2 else 0):
            d = DILS[g]
            span = 128 * d
            KT = B["KT"][g]; Vb = B["Vb"][g]; qT = B["qT"][g]
            for blk in range(ATT_LIMIT["blocks"]):
                u, r = blk // d, blk % d
                qsl = slice(span * u + r, span * u + r + 127 * d + 1, d)
                asl = slice(span * u + r, span * u + r + 127 * d + 1, d)
                bsl = slice(span * (u + 1) + r, span * (u + 1) + r + 127 * d + 1, d)
                vA = u * d + r
                vB = (u + 1) * d + r
                psO = C.psum()
                for hh in range(2):
                    p0 = 64 * hh
                    psS = C.psum()
                    P.op("pe", lambda e, psS=psS, KT=KT, qT=qT, p0=p0, asl=asl, qsl=qsl: e.matmul(
                        psS[:, 0:128], KT[p0:p0 + 64, asl], qT[p0:p0 + 64, qsl], start=True, stop=True),
                        reads=[KT, qT], writes=[psS])
                    P.op("pe", lambda e, psS=psS, KT=KT, qT=qT, p0=p0, bsl=bsl, qsl=qsl: e.matmul(
                        psS[:, 128:256], KT[p0:p0 + 64, bsl], qT[p0:p0 + 64, qsl], start=True, stop=True),
                        reads=[KT, qT], writes=[psS])
                    expS = C.rot("expS", B["expS"]); PT = C.rot("PT", B["PT"])
                    P.op("act", lambda e, psS=psS, expS=expS: e.activation(out=expS[:], in_=psS[:, 0:256], func=AF.Exp),
                         reads=[psS], writes=[expS])
                    gh = g * 16 + 2 * c + hh
                    if sidx == 0 and u == 0:
                        P.op("dve", lambda e, expS=expS, PT=PT, gh=gh: e.scalar_tensor_tensor(
                            out=PT[:, 0:128], in0=expS[:, 0:128], scalar=C.hv[:, 0:1], in1=C.expB[:, gh, 0:128],
                            op0=ALU.mult, op1=ALU.mult), reads=[expS, C.hv, C.expB], writes=[PT])
                        P.op("dve", lambda e, expS=expS, PT=PT, gh=gh: e.tensor_tensor(
                            out=PT[:, 128:256], in0=expS[:, 128:256], in1=C.expB[:, gh, 128:256], op=ALU.mult),
                            reads=[expS, C.expB], writes=[PT])
                    else:
                        eng = "pool" if (blk + hh) % 2 == 0 else "dve"
                        P.op(eng, lambda e, expS=expS, PT=PT, gh=gh: e.tensor_tensor(
                            out=PT[:], in0=expS[:], in1=C.expB[:, gh, :], op=ALU.mult),
                            reads=[expS, C.expB], writes=[PT])
                    for (col0, lhs_fn) in (((0, "V"), (128, "Z")) if ATT_LIMIT["stage"] >= 3 else ()):
                        for half, vb in ((0, vA), (1, vB)):
                            if lhs_fn == "V":
                                P.op("pe", lambda e, psO=psO, Vb=Vb, PT=PT, p0=p0, vb=vb, half=half, col0=col0: e.matmul(
                                    psO[p0:p0 + 64, col0:col0 + 128], Vb[:, vb, p0:p0 + 64], PT[:, half * 128:(half + 1) * 128],
                                    start=(half == 0), stop=(half == 1)), reads=[Vb, PT], writes=[psO])
                            else:
                                P.op("pe", lambda e, psO=psO, PT=PT, p0=p0, half=half, col0=col0: e.matmul(
                                    psO[p0:p0 + 64, col0:col0 + 128], C.ones[:, 0:64], PT[:, half * 128:(half + 1) * 128],
                                    start=(half == 0), stop=(half == 1)), reads=[C.ones, PT], writes=[psO])
                if ATT_LIMIT["stage"] < 4:
                    continue
                pv = lambda psO=psO: psO[:, 0:256].rearrange("p (a q) -> p a q", a=2)
                if g == 0:
                    P.op("dve", lambda e, psO=psO, qsl=qsl: e.tensor_copy(
                        out=NZ[:, :, qsl], in_=psO[:, 0:256].rearrange("p (a q) -> p a q", a=2)), reads=[psO], writes=[NZ])
                else:
                    P.op("dve", lambda e, psO=psO, qsl=qsl: e.tensor_tensor(
                        out=NZ[:, :, qsl], in0=psO[:, 0:256].rearrange("p (a q) -> p a q", a=2), in1=NZ[:, :, qsl], op=ALU.add),
                        reads=[psO, NZ], writes=[NZ])
        if ATT_LIMIT["stage"] < 5:
            continue
        P.op("dve", lambda e: e.reciprocal(out=NZ[:, 1, :], in_=NZ[:, 1, :]), reads=[NZ], writes=[NZ])
        P.op("dve", lambda e, c=c: e.tensor_tensor(out=B["oT"][:, c, :], in0=NZ[:, 0, :], in1=NZ[:, 1, :], op=ALU.mult),
             reads=[NZ], writes=[B["oT"]])


def attn_outproj_tile(C, u, xT, B, wo_d):
    P = C.P
    for m in range(KC):
        if m % 2 == 0:
            wo = C.rot("awo", B["wo"])
            P.dma("sp", lambda e, wo=wo, m=m: e.dma_start(
                out=wo[:], in_=wo_d[:, m * 128:(m + 2) * 128].rearrange("(h p) c -> p h c", p=128)), wo_d, wo, sem_owner=(wo, "w"))
        mm = m % 2
        ps = C.psum()
        for c in range(KC):
            P.op("pe", lambda e, ps=ps, c=c, mm=mm, wo=wo: e.matmul(ps[:], wo[:, c, mm * 128:(mm + 1) * 128], B["oT"][:, c, u * TT:(u + 1) * TT],
                                                          start=(c == 0), stop=(c == KC - 1)), reads=[wo, B["oT"]], writes=[ps])
        P.op("dve", lambda e, ps=ps, m=m: e.tensor_tensor(out=xT[:, m, :], in0=ps[:], in1=xT[:, m, :], op=ALU.add),
             reads=[ps, xT], writes=[xT])

import ml_dtypes
from concourse.bass_utils import run_bass_kernel_spmd

NCORES = 8
T = 4096
NT = T // TT
NCST = 80


def _common_inputs(C, need_onesbd=True):
    P = C.P
    ident_d = P.dram("ident", [128, 128], F32, kind="ExternalInput")
    identb_d = P.dram("identb", [128, 128], BF16, kind="ExternalInput")
    ones_d = P.dram("ones", [128, 128], BF16, kind="ExternalInput")
    onesbd_d = P.dram("onesbd", [128, 128], BF16, kind="ExternalInput")
    cst_d = P.dram("cst", [128, NCST], F32, kind="ExternalInput")
    load_consts(C, ident_d, identb_d, ones_d, onesbd_d)
    C.cst = P.sbuf("cst", [128, NCST], F32)
    P.dma("sp", lambda e: e.dma_start(out=C.cst[:], in_=cst_d[:]), cst_d, C.cst, sem_owner=(C.cst, "w"))


def _cast_all(C, specs):
    P = C.P
    out = {}
    P.begin_scope()
    stage = [(P.sbuf(f"st32_{i}", [128, 2048], F32), P.sbuf(f"st16_{i}", [128, 2048], BF16)) for i in range(3)]
    for name, K, N in specs:
        w_d = P.dram(name, [K, N], F32, kind="ExternalInput")
        wb_d = P.dram(name + "_bf", [K, N], BF16)
        cast_weight(C, w_d, wb_d, K, N, stage)
        out[name] = wb_d
    P.end_scope()
    return out


def _tile_views(P, name, d, ncols_off=0):
    return [P.view(f"{name}_t{t}", d[:, ncols_off + t * TT:ncols_off + (t + 1) * TT]) for t in range(NT)]


def build_stage1():
    nc = bass.Bass("TRN2", target_bir_lowering=False)
    C = Ctx(nc)
    P = C.P
    x_d = P.dram("x", [T, D], F32, kind="ExternalInput")
    lbl_d = P.dram("lbl", [128, 2, 8], F32, kind="ExternalInput")
    mask2_d = P.dram("mask2", [128, 128], F32, kind="ExternalInput")
    xa_d = P.dram("xa_out", [D, T], F32, kind="ExternalOutput")
    sumL_d = P.dram("sumL", [128, NH, 128], F32, kind="ExternalOutput")
    sumD_d = P.dram("sumD", [128, NH], F32, kind="ExternalOutput")
    _common_inputs(C)
    hgrn_consts(C, lbl_d, mask2_d)
    W = _cast_all(C, [("w_in00", D, 2 * DFF), ("w_out00", DFF, D), ("a_w_in", D, 4096)])
    xa_t = _tile_views(P, "xa", xa_d)
    xT = [P.sbuf(f"xT{i}", [128, KC, TT], F32) for i in range(2)]
    xin = [P.sbuf(f"xin{i}", [128, D], F32) for i in range(2)]
    FB = ffn_bufs(C)
    HB = hgrn_bufs(C, False)
    for h in range(NH):
        P.op("dve", lambda e, h=h: e.memset(HB["S"][h][:], 0.0), writes=[HB["S"][h]])
    P.op("dve", lambda e: e.memset(HB["dlog"][:], 0.0), writes=[HB["dlog"]])
    for t in range(NT):
        x = xT[t % 2]
        load_xT(C, x_d, t * TT, x, xin)
        ffn_tile(C, x, (C.cst, 0), W["w_in00"], W["w_out00"], FB)
        store_resid(C, x, xa_t[t])
        rms_norm_T(C, x, (C.cst, 8), FB["hT"], FB["sq"], FB["rstd"])
        hgrn_tile(C, FB["hT"], x, W["a_w_in"], HB, False)
    for h in range(NH):
        P.dma("pool", lambda e, h=h: e.dma_start(out=sumL_d[:, h, :], in_=HB["S"][h][:]), HB["S"][h], sumL_d,
              sem_owner=(HB["S"][h], "r"))
    P.op("act", lambda e: e.activation(out=HB["dlog"][:], in_=HB["dlog"][:], func=AF.Exp), reads=[HB["dlog"]], writes=[HB["dlog"]])
    P.dma("pool", lambda e: e.dma_start(out=sumD_d[:], in_=HB["dlog"][:]), HB["dlog"], sumD_d, sem_owner=(HB["dlog"], "r"))
    st = P.emit(final_bufs=[sumL_d, sumD_d] + xa_t)
    return nc, st


def build_stage2():
    nc = bass.Bass("TRN2", target_bir_lowering=False)
    C = Ctx(nc)
    P = C.P
    xa_in = P.dram("xa_in", [D, T], F32, kind="ExternalInput")
    lbl_d = P.dram("lbl", [128, 2, 8], F32, kind="ExternalInput")
    mask2_d = P.dram("mask2", [128, 128], F32, kind="ExternalInput")
    sumL_all = P.dram("sumL_all", [NCORES, 128, NH, 128], F32, kind="ExternalInput")
    sumD_all = P.dram("sumD_all", [128, NCORES, NH], F32, kind="ExternalInput")
    xa_d = P.dram("xa_out", [D, T], F32, kind="ExternalOutput")
    kTo = P.dram("kT_out", [NG, D, T], BF16, kind="ExternalOutput")
    vto = P.dram("vtok_out", [T, 3 * D], BF16, kind="ExternalOutput")
    xm_d = P.dram("xmid", [D, T], F32)
    _common_inputs(C)
    hgrn_consts(C, lbl_d, mask2_d)
    W = _cast_all(C, [("a_w_in", D, 4096), ("a_w_out", D, D), ("w_in01", D, 2 * DFF), ("w_out01", DFF, D),
                      ("w_kv", D, 6 * D)])
    xin_t = _tile_views(P, "xain", xa_in)
    xm_t = _tile_views(P, "xm", xm_d)
    xa_t = _tile_views(P, "xa", xa_d)
    xT = [P.sbuf(f"xT{i}", [128, KC, TT], F32) for i in range(2)]
    hT = P.sbuf("hT", [128, KC, TT], BF16)
    sq = [P.sbuf(f"sq{i}", [128, TT], BF16) for i in range(2)]
    rstd = P.sbuf("rstd", [128, TT], F32)
    P.begin_scope()
    HB = hgrn_bufs(C, True)
    P.dma("sp", lambda e: e.dma_start(out=HB["wo"][:], in_=W["a_w_out"][:].rearrange("(h p) c -> p h c", p=128)),
          W["a_w_out"], HB["wo"], sem_owner=(HB["wo"], "w"))
    Dall = P.sbuf("Dall", [128, NCORES, NH], F32)
    P.dma("sp", lambda e: e.dma_start(out=Dall[:], in_=sumD_all[:]), sumD_all, Dall, sem_owner=(Dall, "w"))
    mD = P.sbuf("mD", [128, NCORES, NH], F32)
    Lj = [P.sbuf(f"Lj{i}", [128, NH, 128], F32) for i in range(2)]
    Tj = [P.sbuf(f"Tj{i}", [128, NH, 128], F32) for i in range(2)]
    for h in range(NH):
        P.op("dve", lambda e, h=h: e.memset(HB["S"][h][:], 0.0), writes=[HB["S"][h]])
    for j in range(NCORES):
        P.op("dve", lambda e, j=j: e.tensor_scalar(out=mD[:, j, :], in0=Dall[:, j, :], scalar1=C.cst[:, 64 + j:65 + j],
                                                   scalar2=C.cst[:, 72 + j:73 + j], op0=ALU.mult, op1=ALU.add),
             reads=[Dall, C.cst], writes=[mD])
        L = Lj[j % 2]; Tt = Tj[j % 2]
        P.dma("sp", lambda e, L=L, j=j: e.dma_start(out=L[:], in_=sumL_all[j]), sumL_all, L, sem_owner=(L, "w"))
        P.op("dve", lambda e, L=L, Tt=Tt, j=j: e.tensor_scalar(out=Tt[:], in0=L[:], scalar1=C.cst[:, 64 + j:65 + j],
                                                              scalar2=None, op0=ALU.mult), reads=[L, C.cst], writes=[Tt])
        for h in range(NH):
            S = HB["S"][h]
            P.op("dve", lambda e, S=S, Tt=Tt, j=j, h=h: e.scalar_tensor_tensor(
                out=S[:], in0=S[:], scalar=mD[:, j, h:h + 1], in1=Tt[:, h, :], op0=ALU.mult, op1=ALU.add),
                reads=[S, mD, Tt], writes=[S])
    for t in range(NT):
        x = xT[t % 2]
        load_resid(C, xin_t[t], x)
        rms_norm_T(C, x, (C.cst, 8), hT, sq, rstd)
        hgrn_tile(C, hT, x, W["a_w_in"], HB, True, og_col=(C.cst, 56))
        store_resid(C, x, xm_t[t])
    P.end_scope()
    P.begin_scope()
    FB = ffn_bufs(C)
    KB = kv_bufs(C)
    kT_v = [P.view(f"kTo{g}", kTo[g]) for g in range(NG)]
    vt_v = P.view("vto", vto[:])
    for t in range(NT):
        x = xT[t % 2]
        load_resid(C, xm_t[t], x)
        ffn_tile(C, x, (C.cst, 16), W["w_in01"], W["w_out01"], FB)
        store_resid(C, x, xa_t[t])
        rms_norm_T(C, x, (C.cst, 48), FB["hT"], FB["sq"], FB["rstd"])
        kv_tile(C, FB["hT"], W["w_kv"], kT_v, vt_v, t * TT, KB, (C.cst, 57), kv_off=0)
    P.end_scope()
    st = P.emit(final_bufs=kT_v + [vt_v] + xa_t)
    return nc, st


def build_stage3():
    nc = bass.Bass("TRN2", target_bir_lowering=False)
    C = Ctx(nc)
    P = C.P
    xa_in = P.dram("xa_in", [D, T], F32, kind="ExternalInput")
    kT_in = P.dram("kT_in", [NG, D, HALO + T], BF16, kind="ExternalInput")
    vt_in = P.dram("vtok_in", [HALO + T, 3 * D], BF16, kind="ExternalInput")
    bias_d = P.dram("bias", [128, 48, 256], F32, kind="ExternalInput")
    hv_d = P.dram("hv", [128, 1], F32, kind="ExternalInput")
    y_d = P.dram("y", [T, D], F32, kind="ExternalOutput")
    xb_d = P.dram("xb", [D, T], F32)
    _common_inputs(C)
    W = _cast_all(C, [("w_in10", D, 2 * DFF), ("w_out10", DFF, D), ("b_w_q", D, 3 * D), ("b_w_o", D, D),
                      ("w_in11", D, 2 * DFF), ("w_out11", DFF, D)])
    xin_t = _tile_views(P, "xain", xa_in)
    xb_t = _tile_views(P, "xb", xb_d)
    qg = P.sbuf("qg", [128, 3], F32)
    P.op("dve", lambda e: e.tensor_scalar(out=qg[:], in0=C.cst[:, 60:63], scalar1=0.125, scalar2=None, op0=ALU.mult),
         reads=[C.cst], writes=[qg])
    P.begin_scope()
    xT = [P.sbuf(f"xT{i}", [128, KC, TT], F32) for i in range(2)]
    FB = ffn_bufs(C)
    for t in range(NT):
        x = xT[t % 2]
        load_resid(C, xin_t[t], x)
        ffn_tile(C, x, (C.cst, 24), W["w_in10"], W["w_out10"], FB)
        store_resid(C, x, xb_t[t])
    P.end_scope()
    P.begin_scope()
    xT = [P.sbuf(f"xT{i}", [128, KC, TT], F32) for i in range(1)]
    attn_consts(C, bias_d, hv_d)
    AB = attn_bufs(C)
    sq = [P.sbuf(f"sq{i}", [128, TT], BF16) for i in range(2)]
    rstd = P.sbuf("rstd", [128, TT], F32)
    kT_v = [P.view(f"kTi{g}", kT_in[g]) for g in range(NG)]
    vt_v = P.view("vti", vt_in[:])
    for s in range(T // ST):
        for u in range(ST // TT):
            t = s * (ST // TT) + u
            x = xT[0]
            load_resid(C, xb_t[t], x)
            hview = P.view(f"ahT_{s}_{u}", AB["hT"][:, :, u * TT:(u + 1) * TT])
            rms_norm_T(C, x, (C.cst, 32), hview, sq, rstd, extra_w=[AB["hT"]])
        attn_supertile(C, s, W["b_w_q"], kT_v, vt_v, AB, (qg, 0))
        for u in range(ST // TT):
            t = s * (ST // TT) + u
            x = xT[0]
            load_resid(C, xb_t[t], x)
            attn_outproj_tile(C, u, x, AB, W["b_w_o"])
            store_resid(C, x, xb_t[t])
    P.end_scope()
    P.begin_scope()
    xT = [P.sbuf(f"xT{i}", [128, KC, TT], F32) for i in range(2)]
    FB = ffn_bufs(C)
    xout = [P.sbuf(f"xout{i}", [128, D], F32) for i in range(2)]
    y_t = [P.view(f"y_t{t}", y_d[t * TT:(t + 1) * TT, :]) for t in range(NT)]
    for t in range(NT):
        x = xT[t % 2]
        load_resid(C, xb_t[t], x)
        ffn_tile(C, x, (C.cst, 40), W["w_in11"], W["w_out11"], FB)
        store_xT_tokmajor(C, x, y_t[t], 0, xout)
    P.end_scope()
    st = P.emit(final_bufs=y_t)
    return nc, st


def _t5_bucket(dist):
    import math
    dist = np.asarray(dist, np.int32)
    max_exact = 16
    large = max_exact + (np.log(np.maximum(dist, 1) / max_exact) / math.log(2048 / max_exact) * (32 - max_exact)).astype(np.int32)
    large = np.minimum(large, 31)
    return np.where(dist < max_exact, dist, large).astype(np.int32)


def _bias_mats(rel_bias):
    out = np.full((128, 48, 256), -30000.0, np.float32)
    j = np.arange(128)[:, None]
    i = np.arange(128)[None, :]
    for g, dil in enumerate((1, 4, 16)):
        bA = _t5_bucket((128 - (j - i)).clip(0, 128) * dil)
        bB = _t5_bucket((i - j).clip(0, 128) * dil)
        for h in range(16):
            gh = g * 16 + h
            colv = rel_bias[:, gh]
            A = colv[bA]
            Bm = colv[bB]
            out[:, gh, 0:128] = np.where(j >= i, A, np.float32(-30000.0))
            out[:, gh, 128:256] = np.where(j <= i, Bm, np.float32(-30000.0))
    return out


def _consts(inputs, core):
    cst = np.zeros((128, NCST), np.float32)
    ng = inputs["norm_gain"]
    for l in range(2):
        for i in range(3):
            cst[:, (l * 3 + i) * 8:(l * 3 + i) * 8 + 8] = ng[l, i].reshape(8, 128).T
    cst[:, 48:56] = inputs["kv_norm"].reshape(8, 128).T
    cst[:, 56] = inputs["a_out_gain"][0]
    for g in range(3):
        cst[:, 57 + g] = np.tile(inputs["k_gain"][g], 2)
        cst[:, 60 + g] = np.tile(inputs["b_q_gain"][0, g], 2)
    first = (core % 4 == 0)
    cst[:, 63] = 0.0 if first else 1.0
    for j in range(8):
        m = 1.0 if (j // 4 == core // 4 and j < core) else 0.0
        cst[:, 64 + j] = m
        cst[:, 72 + j] = 1.0 - m
    return cst


LAST_STATS = {}


def kernel(x, norm_gain, ffn_w_in, ffn_w_out, a_w_in, a_lb_logits, a_out_gain, a_w_out,
           kv_norm, w_kv, k_gain, b_w_q, b_q_gain, b_w_o, rel_bias):
    inputs = dict(norm_gain=np.asarray(norm_gain, np.float32), kv_norm=np.asarray(kv_norm, np.float32),
                  a_out_gain=np.asarray(a_out_gain, np.float32), k_gain=np.asarray(k_gain, np.float32),
                  b_q_gain=np.asarray(b_q_gain, np.float32))
    x = np.asarray(x, np.float32).reshape(NCORES, T, D)
    ffn_w_in = np.asarray(ffn_w_in, np.float32)
    ffn_w_out = np.asarray(ffn_w_out, np.float32)
    ident = np.eye(128, dtype=np.float32)
    bf = ml_dtypes.bfloat16
    onesbd = np.zeros((128, 128), np.float32)
    onesbd[:64, :64] = 1
    onesbd[64:, 64:] = 1
    s_ = np.arange(128)[:, None]
    t_ = np.arange(128)[None, :]
    mask2 = ((s_ // 64 == t_ // 64) & (s_ <= t_)).astype(np.float32)
    lbl = np.ascontiguousarray(np.asarray(a_lb_logits, np.float32).reshape(2, 8, 128).transpose(2, 0, 1))
    common = lambda c: dict(ident=ident, identb=ident.astype(bf), ones=np.ones((128, 128), bf), onesbd=onesbd.astype(bf),
                            cst=_consts(inputs, c))
    cores = list(range(NCORES))
    nc1, st1 = build_stage1()
    in1 = [dict(common(c), x=x[c], lbl=lbl, mask2=mask2, w_in00=ffn_w_in[0, 0], w_out00=ffn_w_out[0, 0],
                a_w_in=np.asarray(a_w_in[0], np.float32)) for c in cores]
    r1 = run_bass_kernel_spmd(nc1, in1, core_ids=cores).results
    sumL_all = np.stack([np.asarray(r["sumL"]) for r in r1])
    sumD_all = np.ascontiguousarray(np.stack([np.asarray(r["sumD"]) for r in r1]).transpose(1, 0, 2))
    nc2, st2 = build_stage2()
    in2 = [dict(common(c), xa_in=np.asarray(r1[c]["xa_out"]), lbl=lbl, mask2=mask2, sumL_all=sumL_all, sumD_all=sumD_all,
                a_w_in=np.asarray(a_w_in[0], np.float32), a_w_out=np.asarray(a_w_out[0], np.float32),
                w_in01=ffn_w_in[0, 1], w_out01=ffn_w_out[0, 1], w_kv=np.asarray(w_kv, np.float32)) for c in cores]
    r2 = run_bass_kernel_spmd(nc2, in2, core_ids=cores).results
    nc3, st3 = build_stage3()
    bias = _bias_mats(np.asarray(rel_bias, np.float32))
    in3 = []
    for c in cores:
        kT = np.zeros((NG, D, HALO + T), bf)
        vt = np.zeros((HALO + T, 3 * D), bf)
        kT[:, :, HALO:] = np.asarray(r2[c]["kT_out"])
        vt[HALO:] = np.asarray(r2[c]["vtok_out"])
        if c % 4 != 0:
            kT[:, :, :HALO] = np.asarray(r2[c - 1]["kT_out"])[:, :, T - HALO:]
            vt[:HALO] = np.asarray(r2[c - 1]["vtok_out"])[T - HALO:]
        hv = np.full((128, 1), 0.0 if c % 4 == 0 else 1.0, np.float32)
        in3.append(dict(common(c), xa_in=np.asarray(r2[c]["xa_out"]), kT_in=kT, vtok_in=vt, bias=bias, hv=hv,
                        w_in10=ffn_w_in[1, 0], w_out10=ffn_w_out[1, 0], b_w_q=np.asarray(b_w_q[0], np.float32),
                        b_w_o=np.asarray(b_w_o[0], np.float32), w_in11=ffn_w_in[1, 1], w_out11=ffn_w_out[1, 1]))
    r3 = run_bass_kernel_spmd(nc3, in3, core_ids=cores).results
    LAST_STATS.update(st1=st1, st2=st2, st3=st3)
    y = np.stack([np.asarray(r["y"]) for r in r3]).reshape(2, 4 * T, D).astype(np.float32)
    return y
```
